# Optimizing a Trainium2 kernel written in Bass

```python
import math
import jax, jax.numpy as jnp
from jax import lax
import numpy as np

D_MODEL = 1024
BATCH = 16
SEQ = 2048
DEPTH = 4

RWKV_WIDTH = D_MODEL // 2
RWKV_HEAD = 64
RWKV_HEADS = RWKV_WIDTH // RWKV_HEAD
DECAY_LORA = 64
AAA_LORA = 64
GATE_LORA = 128
LNX_EPS = 64e-5
DECAY_SCALE = math.exp(-0.5)
SSM_WIDTH = D_MODEL // 2
SSM_GROUP = 16
SSM_GROUPS = SSM_WIDTH // SSM_GROUP
SSM_STATE = 64
D_FF = 4 * D_MODEL
NORM_EPS = 1e-6
SHIFT_COLS = 3 * RWKV_WIDTH + DECAY_LORA + AAA_LORA + GATE_LORA
IN_COLS = SHIFT_COLS + SSM_WIDTH + 2 * D_MODEL

kernel_name = "rwkv7_s5_gated_hybrid"


def rmsnorm(x, g):
    xf = x.astype(jnp.float32)
    y = xf * lax.rsqrt(jnp.mean(xf * xf, axis=-1, keepdims=True) + NORM_EPS)
    return (y * g.astype(jnp.float32)).astype(x.dtype)


def token_shift(z):
    return jnp.pad(z[:, :-1], ((0, 0), (1, 0), (0, 0)))


def rwkv7_time_mix(zm, k_k, k_a, r_k, w0, w_decay_up, a0, w_aaa_up, w_gate_up,
                   lnx_w, lnx_b, w_proj):
    bsz, seq, _ = zm.shape
    W, H, N = RWKV_WIDTH, RWKV_HEADS, RWKV_HEAD
    f32 = jnp.float32
    r, k, v, wd, ad, gd = jnp.split(
        zm, [W, 2 * W, 3 * W, 3 * W + DECAY_LORA, 3 * W + DECAY_LORA + AAA_LORA], axis=-1)
    w = jnp.exp(-DECAY_SCALE * jax.nn.sigmoid((w0 + jnp.tanh(wd) @ w_decay_up).astype(f32)))
    a = jax.nn.sigmoid(a0 + ad @ w_aaa_up)
    g = jax.nn.sigmoid(gd) @ w_gate_up
    kk = k * k_k
    k_mod = k * (1.0 + (a - 1.0) * k_a)

    heads = lambda t: t.reshape(bsz, seq, H, N).astype(f32)
    rh, kh, vh, wh, ah, kkh = map(heads, (r, k_mod, v, w, a, kk))
    kkh = kkh / jnp.maximum(jnp.sqrt(jnp.sum(kkh * kkh, axis=-1, keepdims=True)), 1e-12)
    tm = lambda t: jnp.swapaxes(t, 0, 1)

    def step(state, inp):
        r_t, w_t, k_t, v_t, kk_t, b_t = inp
        sk = jnp.einsum('bhvk,bhk->bhv', state, kk_t)
        state = (state * w_t[:, :, None, :]
                 - sk[..., None] * b_t[:, :, None, :]
                 + v_t[..., None] * k_t[:, :, None, :])
        return state, jnp.einsum('bhvk,bhk->bhv', state, r_t)

    state0 = jnp.zeros((bsz, H, N, N), f32)
    _, y = lax.scan(step, state0, tuple(map(tm, (rh, wh, kh, vh, kkh, kkh * ah))))
    y = tm(y)
    mu = jnp.mean(y, axis=-1, keepdims=True)
    var = jnp.mean(jnp.square(y - mu), axis=-1, keepdims=True)
    y = ((y - mu) * lax.rsqrt(var + LNX_EPS)).reshape(bsz, seq, W)
    y = y * lnx_w.astype(f32) + lnx_b.astype(f32)
    bonus = (jnp.sum(rh * kh * r_k.astype(f32), axis=-1, keepdims=True) * vh).reshape(bsz, seq, W)
    y = (y + bonus).astype(zm.dtype) * g
    return y @ w_proj


def _complex_scan_op(e1, e2):
    a1r, a1i, b1r, b1i = e1
    a2r, a2i, b2r, b2i = e2
    return (a2r * a1r - a2i * a1i,
            a2r * a1i + a2i * a1r,
            a2r * b1r - a2i * b1i + b2r,
            a2r * b1i + a2i * b1r + b2i)


def s5_ssm(u, a_re, a_im, log_dt, b_re, b_im, c_re, c_im, d_skip, w_glu, b_glu):
    bsz, seq, _ = u.shape
    G, C, P = SSM_GROUPS, SSM_GROUP, SSM_STATE
    f32 = jnp.float32
    uf = u.astype(f32).reshape(bsz, seq, G, C)
    dt = jnp.exp(log_dt.astype(f32))[:, None]
    are = jnp.minimum(a_re.astype(f32), -1e-4)
    aim = a_im.astype(f32)
    mag = jnp.exp(dt * are)
    abar_re = mag * jnp.cos(dt * aim)
    abar_im = mag * jnp.sin(dt * aim)
    den = are * are + aim * aim
    nr = abar_re - 1.0
    coef_re = (nr * are + abar_im * aim) / den
    coef_im = (abar_im * are - nr * aim) / den
    br, bi = b_re.astype(f32), b_im.astype(f32)
    bbar_re = coef_re[..., None] * br - coef_im[..., None] * bi
    bbar_im = coef_re[..., None] * bi + coef_im[..., None] * br
    bu_re = jnp.einsum('gpc,bsgc->bsgp', bbar_re, uf)
    bu_im = jnp.einsum('gpc,bsgc->bsgp', bbar_im, uf)
    a_seq_re = jnp.broadcast_to(abar_re, (1, seq, G, P))
    a_seq_im = jnp.broadcast_to(abar_im, (1, seq, G, P))
    _, _, xr, xi = lax.associative_scan(_complex_scan_op, (a_seq_re, a_seq_im, bu_re, bu_im), axis=1)
    y = (jnp.einsum('gcp,bsgp->bsgc', c_re.astype(f32), xr)
         - jnp.einsum('gcp,bsgp->bsgc', c_im.astype(f32), xi)
         + d_skip.astype(f32).reshape(G, C) * uf)
    y = jax.nn.gelu(y.reshape(bsz, seq, SSM_WIDTH), approximate=False).astype(u.dtype)
    z = y @ w_glu + b_glu
    za, zb = jnp.split(z, 2, axis=-1)
    return za * jax.nn.sigmoid(zb)


def sq_relu_mlp(x, w_up, w_down):
    return jnp.square(jax.nn.relu(x @ w_up)) @ w_down


def setup_inputs(seed: int = 0) -> dict:
    key = jax.random.key(seed)
    ks = iter(jax.random.split(key, 40))
    L, D, W, H, N = DEPTH, D_MODEL, RWKV_WIDTH, RWKV_HEADS, RWKV_HEAD
    G, C, P = SSM_GROUPS, SSM_GROUP, SSM_STATE
    nrm = lambda shape, s: jax.random.normal(next(ks), shape, jnp.float32) * s
    unif = lambda shape, lo, hi: jax.random.uniform(next(ks), shape, jnp.float32, lo, hi)
    return {
        "x": nrm((BATCH, SEQ, D), 1.0),
        "norm1_g": 1.0 + nrm((L, D), 0.02),
        "w_in": nrm((L, D, IN_COLS), D ** -0.5),
        "mu_shift": unif((L, SHIFT_COLS), 0.0, 1.0),
        "k_k": 0.85 + nrm((L, W), 0.05),
        "k_a": 1.0 + nrm((L, W), 0.05),
        "r_k": nrm((L, H, N), 0.1),
        "w0": unif((L, W), -3.0, 3.0),
        "w_decay_up": nrm((L, DECAY_LORA, W), 0.1),
        "a0": nrm((L, W), 0.1),
        "w_aaa_up": nrm((L, AAA_LORA, W), 0.1),
        "w_gate_up": nrm((L, GATE_LORA, W), GATE_LORA ** -0.5),
        "lnx_w": 1.0 + nrm((L, W), 0.02),
        "lnx_b": nrm((L, W), 0.01),
        "w_rwkv_proj": nrm((L, W, D), W ** -0.5),
        "a_re": -0.5 + nrm((L, G, P), 0.01),
        "a_im": jnp.pi * jnp.arange(P, dtype=jnp.float32) + nrm((L, G, P), 0.01),
        "log_dt": unif((L, G), math.log(1e-3), math.log(1e-1)),
        "b_re": nrm((L, G, P, C), (2 * C) ** -0.5),
        "b_im": nrm((L, G, P, C), (2 * C) ** -0.5),
        "c_re": nrm((L, G, C, P), (2 * P) ** -0.5),
        "c_im": nrm((L, G, C, P), (2 * P) ** -0.5),
        "d_skip": nrm((L, SSM_WIDTH), 1.0),
        "w_glu": nrm((L, SSM_WIDTH, 2 * D), SSM_WIDTH ** -0.5),
        "b_glu": nrm((L, 2 * D), 0.01),
        "w_out": nrm((L, D, D), D ** -0.5),
        "norm2_g": 1.0 + nrm((L, D), 0.02),
        "w_ff_up": nrm((L, D, D_FF), D ** -0.5),
        "w_ff_down": nrm((L, D_FF, D), D_FF ** -0.5),
        "norm_f_g": 1.0 + nrm((D,), 0.02),
    }


def reference(x, norm1_g, w_in, mu_shift, k_k, k_a, r_k, w0, w_decay_up, a0, w_aaa_up,
              w_gate_up, lnx_w, lnx_b, w_rwkv_proj, a_re, a_im, log_dt, b_re, b_im, c_re,
              c_im, d_skip, w_glu, b_glu, w_out, norm2_g, w_ff_up, w_ff_down, norm_f_g):
    for l in range(DEPTH):
        xn = rmsnorm(x, norm1_g[l])
        z = xn @ w_in[l]
        z_rwkv, u, gates = jnp.split(z, [SHIFT_COLS, SHIFT_COLS + SSM_WIDTH], axis=-1)
        z_mix = z_rwkv + (token_shift(z_rwkv) - z_rwkv) * mu_shift[l]
        y_a = rwkv7_time_mix(z_mix, k_k[l], k_a[l], r_k[l], w0[l], w_decay_up[l], a0[l],
                             w_aaa_up[l], w_gate_up[l], lnx_w[l], lnx_b[l], w_rwkv_proj[l])
        y_b = s5_ssm(u, a_re[l], a_im[l], log_dt[l], b_re[l], b_im[l], c_re[l], c_im[l],
                     d_skip[l], w_glu[l], b_glu[l])
        g_a, g_b = jnp.split(jax.nn.sigmoid(gates), 2, axis=-1)
        x = x + (g_a * y_a + g_b * y_b) @ w_out[l]
        x = x + sq_relu_mlp(rmsnorm(x, norm2_g[l]), w_ff_up[l], w_ff_down[l])
    return rmsnorm(x, norm_f_g)
```

```python
import math
import numpy as np
from contextlib import ExitStack
import concourse.bass as bass
import concourse.mybir as mybir
from concourse.bass_utils import run_bass_kernel_spmd

F32 = mybir.dt.float32
BF16 = mybir.dt.bfloat16
I32 = mybir.dt.int32
AF = mybir.ActivationFunctionType
ALU = mybir.AluOpType

D = 1024
SEQ = 2048
NTOK = 4096
DEPTH = 4
W = 512
INC = 4352
DFF = 4096
DS = math.exp(-0.5)
LNX_EPS = 64e-5
NORM_EPS = 1e-6
NCOLS = 128
ENGS = ("tensor", "vector", "scalar", "gpsimd", "sync")
CENGS = ("tensor", "vector", "scalar", "gpsimd")

C_MU, C_KK, C_KA, C_RK, C_A0, C_LW, C_LB, C_DSK, C_BGLU, C_G1, C_G2, C_ARE, C_AIM, C_LDT = (
    0, 14, 18, 22, 26, 30, 34, 38, 42, 58, 66, 74, 90, 106)


class Buf:
    __slots__ = ("name", "w", "r", "dsem")

    def __init__(self, name):
        self.name = name
        self.w = None
        self.r = {}
        self.dsem = None


class Tl:
    def __init__(self, ap, b):
        self.ap = ap
        self.b = b

    def __getitem__(self, k):
        return self.ap[k]


class Sched:
    def __init__(self, nc, stack, ndsem=40):
        self.nc = nc
        self.prog = {e: [] for e in ENGS}
        self.cnt = {e: 0 for e in ENGS}
        self.esem = {e: stack.enter_context(nc.semaphore("s_" + e)) for e in ENGS}
        self.seen = {e: {} for e in ENGS}
        self.dpool = [[stack.enter_context(nc.semaphore("d%d" % i)), 0, i] for i in range(ndsem)]
        self.dnext = 0
        self.nwaits = 0
        self.nins = 0

    def _need(self, eng, tickets, force=False):
        for t, is_raw in tickets:
            if t is None:
                continue
            if t[0] == "e":
                _, src, val = t
                if src == eng and not (is_raw or force):
                    continue
                key = ("e", src)
                sem = self.esem[src]
            else:
                rec = t[1]
                val = rec[1]
                key = ("d", rec[2])
                sem = rec[0]
            if self.seen[eng].get(key, 0) >= val:
                continue
            self.seen[eng][key] = val
            self.prog[eng].append(("wait", sem, val))
            self.nwaits += 1

    def _deps(self, eng, reads, writes, force=False):
        tickets = []
        for b in reads:
            tickets.append((b.w, True))
        for b in writes:
            tickets.append((b.w, False))
            for t in b.r.values():
                tickets.append((t, False))
        self._need(eng, tickets, force)

    def op(self, eng, method, *args, reads=(), writes=(), **kw):
        self._deps(eng, reads, writes)
        self.cnt[eng] += 1
        t = ("e", eng, self.cnt[eng])
        self.prog[eng].append(("op", method, args, kw))
        for b in reads:
            b.r[eng] = t
        for b in writes:
            b.w = t
            b.r = {}
        self.nins += 1
        return t

    def dma(self, q, out, in_, reads=(), writes=(), sbuf=None, **kw):
        self._deps(q, reads, writes, force=True)
        if sbuf.dsem is None:
            sbuf.dsem = self.dpool[self.dnext % len(self.dpool)]
            self.dnext += 1
        rec = sbuf.dsem
        rec[1] += 16
        t = ("d", rec)
        self.prog[q].append(("dma", out, in_, kw, rec[0]))
        for b in reads:
            b.r["d%d" % rec[2]] = t
        for b in writes:
            b.w = t
            b.r = {}
        self.nins += 1
        return t

    def barrier(self):
        for e in ENGS:
            tk = [(("e", o, self.cnt[o]), True) for o in CENGS if o != e and self.cnt[o] > 0]
            tk += [(("d", rec), True) for rec in self.dpool if rec[1] > 0]
            self._need(e, tk, True)

    def emit(self):
        nc = self.nc
        prog = self.prog
        esem = self.esem
        with nc.Block() as block:
            def mk(ename):
                def body(eng):
                    for it in prog[ename]:
                        if it[0] == "wait":
                            eng.wait_ge(it[1], it[2])
                        elif it[0] == "op":
                            ins = getattr(eng, it[1])(*it[2], **it[3])
                            ins.then_inc(esem[ename], 1)
                        else:
                            ins = eng.dma_start(out=it[1], in_=it[2], **it[3])
                            ins.then_inc(it[4], 16)
                return body
            block.tensor(mk("tensor"))
            block.vector(mk("vector"))
            block.scalar(mk("scalar"))
            block.gpsimd(mk("gpsimd"))
            block.sync(mk("sync"))


def _view(ap2d, shape):
    if len(shape) == 1:
        return ap2d
    if len(shape) == 2:
        return ap2d.rearrange("p (a b) -> p a b", a=shape[0])
    if len(shape) == 3:
        return ap2d.rearrange("p (a b c) -> p a b c", a=shape[0], b=shape[1])
    raise ValueError(shape)


class KB:
    ARENA_W = 45056

    def __init__(self, nlayers, dbg=()):
        self.nl = nlayers
        self.dbg = set(dbg)
        self.nc = bass.Bass("TRN2", target_bir_lowering=False)
        self.st = ExitStack()

    def dram_in(self, name, shape, dt=F32):
        return self.nc.dram_tensor(name, list(shape), dt, kind="ExternalInput").ap()

    def dram_scr(self, name, shape, dt):
        kind = "ExternalOutput" if name in self.dbg else "Internal"
        return self.nc.dram_tensor(name, list(shape), dt, kind=kind).ap()

    def persist(self, name, shape, dt):
        t = self.st.enter_context(self.nc.sbuf_tensor("sb_" + name, list(shape), dt))
        return Tl(t[:], Buf(name))

    def alloc(self, name, shape, dt):
        n = int(np.prod(shape))
        words = n if dt in (F32, I32) else (n + 1) // 2
        words = (words + 1) // 2 * 2
        assert self.aoff + words <= self.ARENA_W, (name, self.aoff, words)
        ap = self.arena[:, self.aoff:self.aoff + words]
        self.aoff += words
        if dt == BF16:
            ap = ap.bitcast(BF16)[:, 0:n]
        elif dt == I32:
            ap = ap.bitcast(I32)[:, 0:n]
        else:
            ap = ap[:, 0:n]
        return Tl(_view(ap, list(shape)), Buf(name))

    def phase(self):
        self.S.barrier()
        self.aoff = 0
        self.pi = 0

    def psum(self):
        t = self.pb[self.pi % 8]
        self.pi += 1
        return t

    def mm(self, out, outb, lhsT, lb, rhs, rb, start=True, stop=True):
        self.S.op("tensor", "matmul", out, lhsT, rhs, start=start, stop=stop,
                  reads=list(lb) + list(rb), writes=[outb])

    def act(self, out, ob, in_, ib, func, extra=(), **kw):
        self.S.op("scalar", "activation", out, in_, func, reads=list(ib) + list(extra), writes=[ob], **kw)

    def tt(self, out, ob, in0, in1, ib, op, eng="vector"):
        self.S.op(eng, "tensor_tensor", out, in0, in1, op, reads=list(ib), writes=[ob])

    def ts(self, out, ob, in0, s1, s2, op0, op1, ib, eng="vector"):
        self.S.op(eng, "tensor_scalar", out, in0, s1, s2, op0, op1, reads=list(ib), writes=[ob])

    def stt(self, out, ob, in0, scalar, in1, op0, op1, ib):
        self.S.op("vector", "scalar_tensor_tensor", out, in0, scalar, in1, op0, op1, reads=list(ib), writes=[ob])

    def cp(self, out, ob, in_, ib, eng="vector"):
        if eng == "scalar":
            self.S.op("scalar", "activation", out, in_, AF.Copy, reads=list(ib), writes=[ob])
        else:
            self.S.op(eng, "tensor_copy", out, in_, reads=list(ib), writes=[ob])

    def ld(self, t, src, q="sync"):
        self.S.dma(q, t.ap, src, writes=[t.b], sbuf=t.b)

    def ldc(self, t, src):
        self.S.dma("gpsimd", t.ap, src, writes=[t.b], sbuf=t.b)

    def stq(self, dst, ap, b, q="sync"):
        self.S.dma(q, dst, ap, reads=[b], sbuf=b)

    def build(self):
        nc, st = self.nc, self.st
        L = self.nl
        self.S = Sched(nc, st)
        S = self.S
        self.x_in = self.dram_in("x", [NTOK, D])
        self.out = nc.dram_tensor("out", [NTOK, D], F32, kind="ExternalOutput").ap()
        self.w_in = self.dram_in("w_in", [L, D, INC])
        self.cols_d = self.dram_in("cols", [L, 128, NCOLS])
        self.gf_d = self.dram_in("gf", [128, 8])
        self.w0_d = self.dram_in("w0", [L, W])
        self.wdu_d = self.dram_in("wdu", [L, 64, W])
        self.wau_d = self.dram_in("wau", [L, 64, W])
        self.wgu_d = self.dram_in("wgu", [L, 128, W])
        self.wproj_d = self.dram_in("wproj", [L, W, D])
        self.wglu_d = self.dram_in("wglu", [L, W, 2 * D])
        self.wout_d = self.dram_in("wout", [L, D, D])
        self.wup_d = self.dram_in("wup", [L, D, DFF])
        self.wdn_d = self.dram_in("wdn", [L, DFF, D])
        self.btre_d = self.dram_in("btre", [L, 16, 128, 128])
        self.btim_d = self.dram_in("btim", [L, 16, 128, 128])
        self.ctre_d = self.dram_in("ctre", [L, 16, 128, 128])
        self.ctim_d = self.dram_in("ctim", [L, 16, 128, 128])
        self.areR_d = self.dram_in("areR", [L, 2048])
        self.aimR_d = self.dram_in("aimR", [L, 2048])
        self.ldtR_d = self.dram_in("ldtR", [L, 2048])
        self.XT = self.dram_scr("XT", [D, NTOK], F32)
        self.X1 = self.dram_scr("X1", [D, NTOK], F32)
        self.ZM = self.dram_scr("ZM", [1792, NTOK], F32)
        self.U = self.dram_scr("U", [W, NTOK], F32)
        self.GT = self.dram_scr("GT", [2048, NTOK], BF16)
        self.G = self.dram_scr("G", [W, NTOK], BF16)
        self.BON = self.dram_scr("BON", [W, NTOK], F32)
        self.KT = self.dram_scr("KT", [W, NTOK], BF16)
        self.RT = self.dram_scr("RT", [W, NTOK], BF16)
        self.KD = self.dram_scr("KD", [W, NTOK], BF16)
        self.BD = self.dram_scr("BD", [W, NTOK], BF16)
        self.KH = self.dram_scr("KH", [NTOK, W], BF16)
        self.BH = self.dram_scr("BH", [NTOK, W], BF16)
        self.VT = self.dram_scr("VT", [NTOK, W], BF16)
        self.GR = self.dram_scr("GR", [128, 128, 1024], BF16)
        self.YT = self.dram_scr("YT", [W, NTOK], F32)
        self.YB = self.dram_scr("YB", [W, NTOK], BF16)
        self.arena = st.enter_context(nc.sbuf_tensor("arena", [128, self.ARENA_W], F32))
        self.pb = []
        for i in range(8):
            p = st.enter_context(nc.psum_tensor("pb%d" % i, [128, 512], F32))
            self.pb.append(Tl(p[:], Buf("pb%d" % i)))
        self.idf = self.persist("idf", [128, 128], F32)
        self.idb = self.persist("idb", [128, 128], BF16)
        self.onesb = self.persist("onesb", [128, 128], BF16)
        self.blkb = self.persist("blkb", [128, 128], BF16)
        self.blkm = self.persist("blkm", [128, 128], BF16)
        self.onesf = self.persist("onesf", [1, 128], F32)
        self.ms2 = self.persist("ms2", [128, 256], F32)
        self.msl = self.persist("msl", [128, 128], F32)
        self.trie = self.persist("trie", [128, 256], F32)
        self.trir = self.persist("trir", [128, 128], F32)
        self.cols = self.persist("cols", [128, NCOLS], F32)
        self.gf = self.persist("gf", [128, 8], F32)
        self.hal = self.persist("hal", [128, 14], F32)
        self.ptc = self.persist("ptc", [128, 4, 32], F32)
        self.Sf = [self.persist("Sf%d" % i, [128, 64], F32) for i in range(8)]
        self.Sb = [self.persist("Sb%d" % i, [128, 64], BF16) for i in range(8)]
        self.aoff = 0
        self.pi = 0
        self.consts()
        self.phase0()
        for l in range(L):
            self.layer(l)
        self.phaseF()
        S.barrier()
        S.emit()
        return nc

    def consts(self):
        S = self.S
        g = "gpsimd"

        def sel(t, ap, op, val=1.0):
            S.op(g, "memset", ap, val, writes=[t.b])
            S.op(g, "affine_select", ap, ap, pattern=[[-1, 128]], compare_op=op, fill=0.0, base=0,
                 channel_multiplier=1, reads=[t.b], writes=[t.b])
        sel(self.idf, self.idf.ap, ALU.is_equal)
        sel(self.idb, self.idb.ap, ALU.is_equal)
        S.op(g, "memset", self.onesb.ap, 1.0, writes=[self.onesb.b])
        S.op(g, "memset", self.onesf.ap, 1.0, writes=[self.onesf.b])
        for t, v in ((self.blkb, 1.0), (self.blkm, 1.0 / 64)):
            S.op(g, "memset", t.ap, 0.0, writes=[t.b])
            S.op(g, "memset", t[0:64, 0:64], v, writes=[t.b])
            S.op(g, "memset", t[64:128, 64:128], v, writes=[t.b])
        self.ld(self.ms2, self.dram_in("c_ms2", [128, 256]))
        self.ld(self.msl, self.dram_in("c_msl", [128, 128]))
        self.ld(self.trie, self.dram_in("c_trie", [128, 256]))
        self.ld(self.trir, self.dram_in("c_trir", [128, 128]))
        self.ld(self.gf, self.gf_d)

    def phase0(self):
        self.phase()
        xin = [self.alloc("xin%d" % i, [1024], F32) for i in range(2)]
        xo = [self.alloc("xo%d" % i, [8, 128], F32) for i in range(2)]
        XTv = self.XT.rearrange("(c p) n -> p c n", p=128)
        for tb in range(32):
            xi = xin[tb % 2]
            o = xo[tb % 2]
            self.ld(xi, self.x_in[tb * 128:(tb + 1) * 128, :])
            for h in range(2):
                p = self.psum()
                for j in range(4):
                    c = h * 4 + j
                    self.S.op("tensor", "transpose", p[:, j * 128:(j + 1) * 128], xi[:, c * 128:(c + 1) * 128],
                              self.idf.ap, reads=[xi.b, self.idf.b], writes=[p.b])
                self.cp(o[:, h * 4:(h + 1) * 4, :], o.b, p.ap.rearrange("p (a b) -> p a b", a=4), [p.b],
                        eng=("vector" if h == 0 else "scalar"))
            self.stq(XTv[:, :, tb * 128:(tb + 1) * 128], o.ap, o.b)

    def rmsnorm(self, xt, n, gcol0, gt, xn, sq, rs):
        self.act(sq.ap, sq.b, xt.ap, [xt.b], AF.Square)
        p = self.psum()
        for c in range(8):
            self.mm(p[:, 0:n], p.b, self.onesb.ap, [self.onesb.b], sq[:, c, :], [sq.b], start=(c == 0), stop=(c == 7))
        self.act(rs.ap, rs.b, p[:, 0:n], [p.b], AF.Sqrt, bias=NORM_EPS, scale=1.0 / D)
        self.S.op("vector", "reciprocal", rs.ap, rs.ap, reads=[rs.b], writes=[rs.b])
        for c in range(8):
            self.stt(xn[:, c, :], xn.b, xt[:, c, :], gt[:, gcol0 + c:gcol0 + c + 1], rs.ap, ALU.mult, ALU.mult,
                     [xt.b, gt.b, rs.b])

    def layer(self, l):
        self.phaseA(l)
        self.phaseB(l)
        self.phaseC1(l)
        self.phaseC2(l)
        self.phaseD(l)
        self.phaseE1(l)
        self.phaseE2(l)

    def phaseA(self, l):
        S = self.S
        self.phase()
        self.ld(self.cols, self.cols_d[l])
        win = [self.alloc("win%d" % c, [INC], BF16) for c in range(8)]
        for c in range(8):
            self.ldc(win[c], self.w_in[l, c * 128:(c + 1) * 128, :])
        xts = [self.alloc("xt%d" % i, [8, 512], F32) for i in range(2)]
        sq = self.alloc("sq", [8, 512], BF16)
        xn = self.alloc("xn", [8, 512], BF16)
        rs = self.alloc("rs", [512], F32)
        zb = [self.alloc("zb%d" % i, [513], F32) for i in range(2)]
        dtmp = [self.alloc("dtmp%d" % i, [512], F32) for i in range(2)]
        sf = [self.alloc("sf%d" % i, [512], F32) for i in range(3)]
        sbf = [self.alloc("sbf%d" % i, [512], BF16) for i in range(3)]
        XTv = self.XT.rearrange("(c p) n -> p c n", p=128)
        ZMv = self.ZM.rearrange("(c p) n -> p c n", p=128)
        Uv = self.U.rearrange("(c p) n -> p c n", p=128)
        GTv = self.GT.rearrange("(c p) n -> p c n", p=128)
        k = 0
        for ti in range(8):
            n0 = ti * 512
            xt = xts[ti % 2]
            self.ld(xt, XTv[:, :, n0:n0 + 512])
            self.rmsnorm(xt, 512, C_G1, self.cols, xn, sq, rs)
            for j in range(34):
                p = self.psum()
                for c in range(8):
                    self.mm(p.ap, p.b, win[c][:, j * 128:(j + 1) * 128], [win[c].b], xn[:, c, :], [xn.b],
                            start=(c == 0), stop=(c == 7))
                if j < 14:
                    z = zb[j % 2]
                    d = dtmp[j % 2]
                    o = sf[k % 3]
                    k += 1
                    self.act(z[:, 1:513], z.b, p.ap, [p.b], AF.Copy)
                    if ti % 4 == 0:
                        S.op("gpsimd", "memset", z[:, 0:1], 0.0, writes=[z.b])
                    else:
                        self.cp(z[:, 0:1], z.b, self.hal[:, j:j + 1], [self.hal.b], eng="gpsimd")
                    self.tt(d.ap, d.b, z[:, 0:512], z[:, 1:513], [z.b], ALU.subtract)
                    self.stt(o.ap, o.b, d.ap, self.cols[:, C_MU + j:C_MU + j + 1], z[:, 1:513], ALU.mult, ALU.add,
                             [d.b, z.b, self.cols.b])
                    self.cp(self.hal[:, j:j + 1], self.hal.b, z[:, 512:513], [z.b], eng="gpsimd")
                    self.stq(ZMv[:, j, n0:n0 + 512], o.ap, o.b)
                elif j < 18:
                    o = sf[k % 3]
                    k += 1
                    self.act(o.ap, o.b, p.ap, [p.b], AF.Copy)
                    self.stq(Uv[:, j - 14, n0:n0 + 512], o.ap, o.b)
                else:
                    o = sbf[k % 3]
                    k += 1
                    self.act(o.ap, o.b, p.ap, [p.b], AF.Sigmoid)
                    self.stq(GTv[:, j - 18, n0:n0 + 512], o.ap, o.b)

    def phaseB(self, l):
        S = self.S
        self.phase()
        cols = self.cols
        wdu = self.alloc("wdu", [W], BF16)
        wau = self.alloc("wau", [W], BF16)
        wgu = self.alloc("wgu", [W], BF16)
        w0r = self.alloc("w0r", [W], F32)
        S.dma("gpsimd", wdu[0:64, :], self.wdu_d[l], writes=[wdu.b], sbuf=wdu.b)
        S.dma("gpsimd", wau[64:128, :], self.wau_d[l], writes=[wau.b], sbuf=wau.b)
        self.ldc(wgu, self.wgu_d[l])
        S.dma("sync", w0r[0:1, :], self.w0_d[l:l + 1, :], writes=[w0r.b], sbuf=w0r.b)
        zt = self.alloc("zt", [14, 512], F32)
        twd = self.alloc("twd", [512], BF16)
        adb = self.alloc("adb", [512], BF16)
        sgd = self.alloc("sgd", [512], BF16)
        sw = [self.alloc("sw%d" % i, [512], F32) for i in range(4)]
        aa = [self.alloc("aa%d" % i, [512], F32) for i in range(4)]
        km = [self.alloc("km%d" % i, [512], F32) for i in range(4)]
        bb = [self.alloc("bb%d" % i, [512], F32) for i in range(4)]
        gout = [self.alloc("gout%d" % i, [512], BF16) for i in range(2)]
        sqk = [self.alloc("sqk%d" % i, [512], BF16) for i in range(4)]
        nrm = [self.alloc("nrm%d" % i, [512], F32) for i in range(4)]
        kap = [self.alloc("kap%d" % i, [512], F32) for i in range(4)]
        tq = [self.alloc("tq%d" % i, [512], F32) for i in range(4)]
        rkb = [self.alloc("rkb%d" % i, [512], BF16) for i in range(4)]
        bon = [self.alloc("bon%d" % i, [512], F32) for i in range(4)]
        e12 = [self.alloc("e12_%d" % i, [256], F32) for i in range(2)]
        eng_ = [self.alloc("eneg%d" % i, [128], F32) for i in range(2)]
        okt = [self.alloc("okt%d" % i, [512], BF16) for i in range(2)]
        ort = [self.alloc("ort%d" % i, [512], BF16) for i in range(2)]
        okd = [self.alloc("okd%d" % i, [512], BF16) for i in range(2)]
        obd = [self.alloc("obd%d" % i, [512], BF16) for i in range(2)]
        eq = [self.alloc("eq%d" % i, [512], F32) for i in range(2)]
        okh = [self.alloc("okh%d" % i, [512], BF16) for i in range(2)]
        obh = [self.alloc("obh%d" % i, [512], BF16) for i in range(2)]
        ovt = [self.alloc("ovt%d" % i, [512], BF16) for i in range(2)]
        ZMv = self.ZM.rearrange("(c p) n -> p c n", p=128)
        Gv = self.G.rearrange("(c p) n -> p c n", p=128)
        for ti in range(8):
            n0 = ti * 512
            self.ld(zt, ZMv[:, :, n0:n0 + 512])
            self.act(twd[0:64, :], twd.b, zt[0:64, 12, :], [zt.b], AF.Tanh)
            self.cp(adb[64:128, :], adb.b, zt[64:128, 12, :], [zt.b], eng="vector")
            self.act(sgd.ap, sgd.b, zt[:, 13, :], [zt.b], AF.Sigmoid)
            for tc in range(4):
                p = self.psum()
                self.mm(p.ap, p.b, twd[0:64, tc * 128:(tc + 1) * 128], [twd.b], wdu[0:64, :], [wdu.b], True, False)
                self.mm(p.ap, p.b, self.onesf[0:1, :], [self.onesf.b], w0r[0:1, :], [w0r.b], False, True)
                self.act(sw[tc].ap, sw[tc].b, p.ap, [p.b], AF.Sigmoid)
            for hp in range(4):
                cs = slice(hp * 128, (hp + 1) * 128)
                p = self.psum()
                self.mm(p.ap, p.b, wau[64:128, cs], [wau.b], adb[64:128, :], [adb.b])
                self.act(aa[hp].ap, aa[hp].b, p.ap, [p.b], AF.Sigmoid, extra=[cols.b],
                         bias=cols[:, C_A0 + hp:C_A0 + hp + 1])
                p = self.psum()
                self.mm(p.ap, p.b, wgu[:, cs], [wgu.b], sgd.ap, [sgd.b])
                go = gout[hp % 2]
                self.act(go.ap, go.b, p.ap, [p.b], AF.Copy)
                self.stq(Gv[:, hp, n0:n0 + 512], go.ap, go.b)
            R_ = [zt[:, hp, :] for hp in range(4)]
            K_ = [zt[:, 4 + hp, :] for hp in range(4)]
            V_ = [zt[:, 8 + hp, :] for hp in range(4)]
            kkc = [cols[:, C_KK + hp:C_KK + hp + 1] for hp in range(4)]
            for hp in range(4):
                self.act(sqk[hp].ap, sqk[hp].b, K_[hp], [zt.b], AF.Square, extra=[cols.b], scale=kkc[hp])
            for hp in range(4):
                p = self.psum()
                self.mm(p.ap, p.b, self.blkb.ap, [self.blkb.b], sqk[hp].ap, [sqk[hp].b])
                self.act(nrm[hp].ap, nrm[hp].b, p.ap, [p.b], AF.Sqrt)
            for hp in range(4):
                self.ts(nrm[hp].ap, nrm[hp].b, nrm[hp].ap, 1e-12, None, ALU.max, ALU.bypass, [nrm[hp].b])
                S.op("vector", "reciprocal", nrm[hp].ap, nrm[hp].ap, reads=[nrm[hp].b], writes=[nrm[hp].b])
            for hp in range(4):
                self.stt(kap[hp].ap, kap[hp].b, K_[hp], kkc[hp], nrm[hp].ap, ALU.mult, ALU.mult, [zt.b, cols.b, nrm[hp].b])
                self.ts(tq[hp].ap, tq[hp].b, aa[hp].ap, -1.0, cols[:, C_KA + hp:C_KA + hp + 1], ALU.add, ALU.mult,
                        [aa[hp].b, cols.b])
            for hp in range(4):
                self.tt(bb[hp].ap, bb[hp].b, kap[hp].ap, aa[hp].ap, [kap[hp].b, aa[hp].b], ALU.mult, eng="gpsimd")
                self.stt(km[hp].ap, km[hp].b, tq[hp].ap, 1.0, K_[hp], ALU.add, ALU.mult, [tq[hp].b, zt.b])
            for hp in range(4):
                self.stt(rkb[hp].ap, rkb[hp].b, R_[hp], cols[:, C_RK + hp:C_RK + hp + 1], km[hp].ap, ALU.mult, ALU.mult,
                         [zt.b, cols.b, km[hp].b])
            for hp in range(4):
                p = self.psum()
                self.mm(p.ap, p.b, self.blkb.ap, [self.blkb.b], rkb[hp].ap, [rkb[hp].b])
                bo = bon[hp]
                self.tt(bo.ap, bo.b, p.ap, V_[hp], [p.b, zt.b], ALU.mult)
                self.stq(self.BON[hp * 128:(hp + 1) * 128, n0:n0 + 512], bo.ap, bo.b)
            for hp in range(4):
                o1, o2, o3, o4 = okt[hp % 2], ort[hp % 2], okd[hp % 2], obd[hp % 2]
                for tc in range(4):
                    ts_ = slice(tc * 128, (tc + 1) * 128)
                    p = self.psum()
                    self.mm(p[:, 0:256], p.b, sw[tc][:, hp * 128:(hp + 1) * 128], [sw[tc].b], self.trie.ap, [self.trie.b])
                    e = e12[tc % 2]
                    en = eng_[tc % 2]
                    self.act(e.ap, e.b, p[:, 0:256], [p.b], AF.Exp)
                    self.act(en.ap, en.b, p[:, 0:128], [p.b], AF.Exp, scale=-1.0)
                    self.tt(o1[:, ts_], o1.b, kap[hp][:, ts_], e[:, 128:256], [kap[hp].b, e.b], ALU.mult)
                    self.tt(o2[:, ts_], o2.b, R_[hp][:, ts_], e[:, 0:128], [zt.b, e.b], ALU.mult, eng="gpsimd")
                    self.tt(o3[:, ts_], o3.b, km[hp][:, ts_], en.ap, [km[hp].b, en.b], ALU.mult)
                    self.tt(o4[:, ts_], o4.b, bb[hp][:, ts_], en.ap, [bb[hp].b, en.b], ALU.mult, eng="gpsimd")
                    self.cp(self.ptc[:, hp, ti * 4 + tc:ti * 4 + tc + 1], self.ptc.b, e[:, 127:128], [e.b], eng="gpsimd")
                rows = slice(hp * 128, (hp + 1) * 128)
                self.stq(self.KT[rows, n0:n0 + 512], o1.ap, o1.b)
                self.stq(self.RT[rows, n0:n0 + 512], o2.ap, o2.b)
                self.stq(self.KD[rows, n0:n0 + 512], o3.ap, o3.b)
                self.stq(self.BD[rows, n0:n0 + 512], o4.ap, o4.b)
            for tc in range(4):
                ts_ = slice(tc * 128, (tc + 1) * 128)
                q = eq[tc % 2]
                p = self.psum()
                self.mm(p.ap, p.b, self.trir.ap, [self.trir.b], sw[tc].ap, [sw[tc].b])
                self.act(q.ap, q.b, p.ap, [p.b], AF.Exp)
                for srcs, dst, dram, useq in ((km, okh[tc % 2], self.KH, True), (bb, obh[tc % 2], self.BH, True),
                                              (None, ovt[tc % 2], self.VT, False)):
                    p = self.psum()
                    for hp in range(4):
                        if srcs is None:
                            ia, ib = zt[:, 8 + hp, ts_], zt.b
                        else:
                            ia, ib = srcs[hp][:, ts_], srcs[hp].b
                        S.op("tensor", "transpose", p[:, hp * 128:(hp + 1) * 128], ia, self.idf.ap,
                             reads=[ib, self.idf.b], writes=[p.b])
                    if useq:
                        self.tt(dst.ap, dst.b, p.ap, q.ap, [p.b, q.b], ALU.mult)
                    else:
                        self.act(dst.ap, dst.b, p.ap, [p.b], AF.Copy)
                    self.stq(dram[n0 + tc * 128:n0 + (tc + 1) * 128, :], dst.ap, dst.b)

    def phaseC1(self, l):
        S = self.S
        self.phase()
        KI = 8
        kr = [self.alloc("kr%d" % i, [2, 128], BF16) for i in range(KI)]
        kd = [self.alloc("kd%d" % i, [128], BF16) for i in range(KI)]
        bd = [self.alloc("bd%d" % i, [128], BF16) for i in range(KI)]
        xx = [[self.alloc("xx%d_%d" % (i, j), [4, 128], BF16) for j in range(2)] for i in range(KI)]
        mf = [self.alloc("mf%d" % i, [2, 128], F32) for i in range(KI)]
        mb = [[self.alloc("mb%d_%d" % (i, j), [2, 128], BF16) for j in range(2)] for i in range(KI)]
        outp = [self.alloc("outp%d" % i, [2, 4, 128], BF16) for i in range(2 * KI)]
        grp = 0
        for c in range(16):
            items = [(sb, hp) for sb in range(2) for hp in range(4)]
            O_ = [outp[(grp % 2) * KI + i] for i in range(KI)]
            grp += 1
            for i, (sb, hp) in enumerate(items):
                ncol = slice(sb * SEQ + c * 128, sb * SEQ + (c + 1) * 128)
                rows = slice(hp * 128, (hp + 1) * 128)
                K_, D_, B_ = kr[i], kd[i], bd[i]
                S.dma("sync", K_[:, 0, :], self.KT[rows, ncol], writes=[K_.b], sbuf=K_.b)
                S.dma("sync", K_[:, 1, :], self.RT[rows, ncol], writes=[K_.b], sbuf=K_.b)
                self.ld(D_, self.KD[rows, ncol])
                self.ld(B_, self.BD[rows, ncol])
            for e in range(2):
                pr = slice(e * 64, (e + 1) * 64)
                for i in range(KI):
                    K_, D_, B_, X, O, MF = kr[i], kd[i], bd[i], xx[i][0], O_[i], mf[i]
                    krf = K_[pr, :, :].rearrange("p a b -> p (a b)")
                    p1 = self.psum()
                    self.mm(p1[:, 0:256], p1.b, B_[pr, :], [B_.b], krf, [K_.b])
                    p2 = self.psum()
                    self.mm(p2[:, 0:256], p2.b, D_[pr, :], [D_.b], krf, [K_.b])
                    p3 = self.psum()
                    self.mm(p3[:, 0:128], p3.b, K_[pr, 0, :], [K_.b], B_[pr, :], [B_.b])
                    self.tt(X[:, 2 * e, :], X.b, p1[:, 0:128], self.ms2[:, 0:128], [p1.b, self.ms2.b], ALU.mult)
                    self.tt(O[:, e, 2, :], O.b, p1[:, 128:256], self.ms2[:, 128:256], [p1.b, self.ms2.b], ALU.mult)
                    self.tt(O[:, e, 0:2, :], O.b, p2[:, 0:256].rearrange("p (a b) -> p a b", a=2),
                            self.ms2.ap.rearrange("p (a b) -> p a b", a=2), [p2.b, self.ms2.b], ALU.mult)
                    self.tt(X[:, 2 * e + 1, :], X.b, p3[:, 0:128], self.msl.ap, [p3.b, self.msl.b], ALU.mult)
                    self.stt(MF[:, e, :], MF.b, X[:, 2 * e, :], -1.0, self.idf.ap, ALU.mult, ALU.add, [X.b, self.idf.b])
            for i in range(KI):
                self.cp(mb[i][0].ap, mb[i][0].b, mf[i].ap, [mf[i].b], eng=("gpsimd" if i % 2 else "scalar"))
            for j in range(6):
                for i in range(KI):
                    X, Xn = xx[i][j % 2], xx[i][(j + 1) % 2]
                    px = self.psum()
                    for e in range(2):
                        self.mm(px[:, (2 * e) * 128:(2 * e + 1) * 128], px.b, X[:, 2 * e + 1, :], [X.b], X[:, 2 * e, :], [X.b])
                        self.mm(px[:, (2 * e + 1) * 128:(2 * e + 2) * 128], px.b, X[:, 2 * e, :], [X.b], X[:, 2 * e + 1, :], [X.b])
                    self.cp(Xn.ap.rearrange("p a b -> p (a b)"), Xn.b, px.ap, [px.b], eng=("scalar" if i % 2 == 0 else "vector"))
                for i in range(KI):
                    Xn, M, MF = xx[i][(j + 1) % 2], mb[i][j % 2], mf[i]
                    pm = self.psum()
                    for e in range(2):
                        self.mm(pm[:, e * 128:(e + 1) * 128], pm.b, Xn[:, 2 * e + 1, :], [Xn.b], M[:, e, :], [M.b])
                    self.tt(MF.ap.rearrange("p a b -> p (a b)"), MF.b, MF.ap.rearrange("p a b -> p (a b)"),
                            pm[:, 0:256], [MF.b, pm.b], ALU.add)
                    if j < 5:
                        Mn = mb[i][(j + 1) % 2]
                        self.cp(Mn.ap, Mn.b, MF.ap, [MF.b], eng=("gpsimd" if i % 2 else "scalar"))
                    else:
                        self.cp(O_[i][:, :, 3, :], O_[i].b, MF.ap, [MF.b], eng=("gpsimd" if i % 2 else "scalar"))
            for i, (sb, hp) in enumerate(items):
                gi = (sb * 16 + c) * 4 + hp
                self.stq(self.GR[gi], O_[i].ap.rearrange("p a b c -> p (a b c)"), O_[i].b)

    def phaseC2(self, l):
        S = self.S
        self.phase()
        NB = 16
        gr = [self.alloc("gr%d" % i, [2, 4, 128], BF16) for i in range(NB)]
        kr = [self.alloc("kr%d" % i, [2, 128], BF16) for i in range(NB)]
        vt = [self.alloc("vt%d" % i, [128], BF16) for i in range(NB)]
        kh = [self.alloc("kh%d" % i, [128], BF16) for i in range(NB)]
        bh = [self.alloc("bh%d" % i, [128], BF16) for i in range(NB)]
        wb = [self.alloc("wb%d" % i, [2, 64], BF16) for i in range(NB)]
        nu = [self.alloc("nu%d" % i, [2, 64], BF16) for i in range(NB)]
        yo = [self.alloc("yo%d" % i, [128], F32) for i in range(NB)]
        for i in range(8):
            S.op("gpsimd", "memset", self.Sf[i].ap, 0.0, writes=[self.Sf[i].b])
            S.op("gpsimd", "memset", self.Sb[i].ap, 0.0, writes=[self.Sb[i].b])
        items = [(sb, hp) for sb in range(2) for hp in range(4)]
        def loads(c):
            base = (c % 2) * 8
            for i, (sb, hp) in enumerate(items):
                b = base + i
                ncol = slice(sb * SEQ + c * 128, sb * SEQ + (c + 1) * 128)
                rows = slice(hp * 128, (hp + 1) * 128)
                gi = (sb * 16 + c) * 4 + hp
                self.ld(gr[b], self.GR[gi].rearrange("p (a b c) -> p a b c", a=2, b=4))
                S.dma("sync", kr[b][:, 0, :], self.KT[rows, ncol], writes=[kr[b].b], sbuf=kr[b].b)
                S.dma("sync", kr[b][:, 1, :], self.RT[rows, ncol], writes=[kr[b].b], sbuf=kr[b].b)
                self.ld(vt[b], self.VT[ncol, rows], q="scalar")
                self.ld(kh[b], self.KH[ncol, rows], q="scalar")
                self.ld(bh[b], self.BH[ncol, rows], q="scalar")
        loads(0)
        for c in range(16):
            base = (c % 2) * 8
            if c + 1 < 16:
                loads(c + 1)
            pws = []
            for i, (sb, hp) in enumerate(items):
                b = base + i
                G_, K_, V_ = gr[b], kr[b], vt[b]
                Sb = self.Sb[sb * 4 + hp]
                pw = self.psum()
                for e in range(2):
                    pr = slice(e * 64, (e + 1) * 64)
                    fs = slice(e * 64, (e + 1) * 64)
                    self.mm(pw[:, fs], pw.b, K_[pr, 0, :], [K_.b], Sb[pr, :], [Sb.b], True, False)
                    self.mm(pw[:, fs], pw.b, G_[:, e, 0, :], [G_.b], V_[:, fs], [V_.b], False, True)
                self.act(wb[b].ap.rearrange("p a b -> p (a b)"), wb[b].b, pw[:, 0:128], [pw.b], AF.Copy)
            for i, (sb, hp) in enumerate(items):
                b = base + i
                G_, W_, U_ = gr[b], wb[b], nu[b]
                pu = self.psum()
                for e in range(2):
                    fs = slice(e * 64, (e + 1) * 64)
                    self.mm(pu[:, fs], pu.b, G_[:, e, 3, :], [G_.b], W_[:, e, :], [W_.b])
                self.ts(U_.ap.rearrange("p a b -> p (a b)"), U_.b, pu[:, 0:128], -1.0, None, ALU.mult, ALU.bypass, [pu.b])
            for i, (sb, hp) in enumerate(items):
                b = base + i
                G_, K_, V_, H_, B_, U_, Y_ = gr[b], kr[b], vt[b], kh[b], bh[b], nu[b], yo[b]
                Sf, Sb = self.Sf[sb * 4 + hp], self.Sb[sb * 4 + hp]
                ncol = slice(sb * SEQ + c * 128, sb * SEQ + (c + 1) * 128)
                rows = slice(hp * 128, (hp + 1) * 128)
                py = self.psum()
                for e in range(2):
                    pr = slice(e * 64, (e + 1) * 64)
                    fs = slice(e * 64, (e + 1) * 64)
                    self.mm(py[pr, 0:128], py.b, Sb[pr, :], [Sb.b], K_[pr, 1, :], [K_.b], True, False)
                    self.mm(py[pr, 0:128], py.b, V_[:, fs], [V_.b], G_[:, e, 1, :], [G_.b], False, False)
                    self.mm(py[pr, 0:128], py.b, U_[:, e, :], [U_.b], G_[:, e, 2, :], [G_.b], False, True)
                pS = self.psum()
                for e in range(2):
                    pr = slice(e * 64, (e + 1) * 64)
                    fs = slice(e * 64, (e + 1) * 64)
                    self.mm(pS[pr, 0:64], pS.b, H_[:, fs], [H_.b], V_[:, fs], [V_.b], True, False)
                    self.mm(pS[pr, 0:64], pS.b, B_[:, fs], [B_.b], U_[:, e, :], [U_.b], False, True)
                self.act(Y_.ap, Y_.b, py[:, 0:128], [py.b], AF.Copy)
                self.stq(self.YT[rows, ncol], Y_.ap, Y_.b)
                ci = sb * 16 + c
                self.stt(Sf.ap, Sf.b, Sf.ap, self.ptc[:, hp, ci:ci + 1], pS[:, 0:64], ALU.mult, ALU.add,
                         [Sf.b, self.ptc.b, pS.b])
                self.cp(Sb.ap, Sb.b, Sf.ap, [Sf.b], eng="gpsimd")

    def sincos(self, th, n, name, shift, q=None, ki=None):
        if q is None:
            q = self.alloc(name + "q", [n], F32)
            ki = self.alloc(name + "k", [n], I32)
        o = self.alloc(name + "o", [n], F32)
        self.ts(q.ap, q.b, th.ap, shift, 1.0 / (2 * math.pi), ALU.add, ALU.mult, [th.b])
        self.cp(ki.ap, ki.b, q.ap, [q.b])
        self.cp(q.ap, q.b, ki.ap, [ki.b])
        self.stt(o.ap, o.b, q.ap, -2 * math.pi, th.ap, ALU.mult, ALU.add, [q.b, th.b])
        self.ts(o.ap, o.b, o.ap, shift, None, ALU.add, ALU.bypass, [o.b])
        self.ts(o.ap, o.b, o.ap, -math.pi, math.pi, ALU.max, ALU.min, [o.b])
        self.act(o.ap, o.b, o.ap, [o.b], AF.Sin)
        return o

    def phaseD(self, l):
        S = self.S
        self.phase()
        cols = self.cols
        N = 2048
        v3 = lambda t: t.ap.rearrange("p (g q) -> p g q", g=16)
        wbx = self.alloc("wbx", [16, 4, 128], BF16)
        cx = self.alloc("cx", [16, 3, 128], BF16)
        ctab = self.alloc("ctab", [16, 128], F32)
        stab = self.alloc("stab", [16, 128], F32)
        dtc = self.alloc("dtc", [16], F32)
        arc = self.alloc("arc", [16], F32)
        magc = self.alloc("magc", [16], F32)
        thc = self.alloc("thc", [16], F32)
        self.act(dtc.ap, dtc.b, cols[:, C_LDT:C_LDT + 16], [cols.b], AF.Exp)
        self.ts(arc.ap, arc.b, cols[:, C_ARE:C_ARE + 16], -1e-4, None, ALU.min, ALU.bypass, [cols.b])
        self.tt(magc.ap, magc.b, dtc.ap, arc.ap, [dtc.b, arc.b], ALU.mult)
        self.act(magc.ap, magc.b, magc.ap, [magc.b], AF.Exp)
        self.tt(thc.ap, thc.b, dtc.ap, cols[:, C_AIM:C_AIM + 16], [dtc.b, cols.b], ALU.mult)
        s1 = self.sincos(thc, 16, "snC", 0.0)
        c1 = self.sincos(thc, 16, "csC", math.pi / 2)
        self.ts(s1.ap, s1.b, s1.ap, -1.0, None, ALU.mult, ALU.bypass, [s1.b])
        S.op("vector", "memset", ctab[:, :, 0:1], 1.0, writes=[ctab.b])
        S.op("vector", "memset", stab[:, :, 0:1], 0.0, writes=[stab.b])
        tA = self.alloc("tA", [16, 64], F32)
        tB = self.alloc("tB", [16, 64], F32)
        c2 = self.alloc("c2", [16], F32)
        s2 = self.alloc("s2", [16], F32)
        u1 = self.alloc("u1", [16], F32)
        Lh = 1
        while Lh <= 64:
            cb = c1.ap.unsqueeze(2).to_broadcast([128, 16, Lh])
            sbq = s1.ap.unsqueeze(2).to_broadcast([128, 16, Lh])
            self.tt(tA[:, :, 0:Lh], tA.b, ctab[:, :, 0:Lh], cb, [ctab.b, c1.b], ALU.mult)
            self.tt(tB[:, :, 0:Lh], tB.b, stab[:, :, 0:Lh], sbq, [stab.b, s1.b], ALU.mult)
            self.tt(ctab[:, :, Lh:2 * Lh], ctab.b, tA[:, :, 0:Lh], tB[:, :, 0:Lh], [tA.b, tB.b], ALU.subtract)
            self.tt(tA[:, :, 0:Lh], tA.b, ctab[:, :, 0:Lh], sbq, [ctab.b, s1.b], ALU.mult)
            self.tt(tB[:, :, 0:Lh], tB.b, stab[:, :, 0:Lh], cb, [stab.b, c1.b], ALU.mult)
            self.tt(stab[:, :, Lh:2 * Lh], stab.b, tA[:, :, 0:Lh], tB[:, :, 0:Lh], [tA.b, tB.b], ALU.add)
            self.tt(c2.ap, c2.b, c1.ap, c1.ap, [c1.b], ALU.mult)
            self.tt(u1.ap, u1.b, s1.ap, s1.ap, [s1.b], ALU.mult)
            self.tt(c2.ap, c2.b, c2.ap, u1.ap, [c2.b, u1.b], ALU.subtract)
            self.tt(s2.ap, s2.b, c1.ap, s1.ap, [c1.b, s1.b], ALU.mult)
            self.ts(s2.ap, s2.b, s2.ap, 2.0, None, ALU.mult, ALU.bypass, [s2.b])
            self.cp(c1.ap, c1.b, c2.ap, [c2.b])
            self.cp(s1.ap, s1.b, s2.ap, [s2.b])
            Lh *= 2
        mark = self.aoff
        def rowld(name, src):
            t = self.alloc(name, [N], F32)
            self.ld(t, src.partition_broadcast(128))
            return t
        are = rowld("areR", self.areR_d[l])
        aim = rowld("aimR", self.aimR_d[l])
        ldt = rowld("ldtR", self.ldtR_d[l])
        self.act(ldt.ap, ldt.b, ldt.ap, [ldt.b], AF.Exp)
        self.ts(are.ap, are.b, are.ap, -1e-4, None, ALU.min, ALU.bypass, [are.b])
        mag = self.alloc("magR", [N], F32)
        self.tt(mag.ap, mag.b, ldt.ap, are.ap, [ldt.b, are.b], ALU.mult)
        self.act(mag.ap, mag.b, mag.ap, [mag.b], AF.Exp)
        th = self.alloc("thR", [N], F32)
        self.tt(th.ap, th.b, ldt.ap, aim.ap, [ldt.b, aim.b], ALU.mult)
        scq = self.alloc("scq", [N], F32)
        sck = self.alloc("sck", [N], I32)
        sn = self.sincos(th, N, "snR", 0.0, scq, sck)
        cs = self.sincos(th, N, "csR", math.pi / 2, scq, sck)
        self.tt(cs.ap, cs.b, cs.ap, mag.ap, [cs.b, mag.b], ALU.mult)
        self.tt(sn.ap, sn.b, sn.ap, mag.ap, [sn.b, mag.b], ALU.mult)
        self.ts(cs.ap, cs.b, cs.ap, -1.0, None, ALU.add, ALU.bypass, [cs.b])
        den = mag
        t1 = th
        self.tt(den.ap, den.b, are.ap, are.ap, [are.b], ALU.mult)
        self.tt(t1.ap, t1.b, aim.ap, aim.ap, [aim.b], ALU.mult)
        self.tt(den.ap, den.b, den.ap, t1.ap, [den.b, t1.b], ALU.add)
        S.op("vector", "reciprocal", den.ap, den.ap, reads=[den.b], writes=[den.b])
        cre = self.alloc("creR", [N], F32)
        cim = self.alloc("cimR", [N], F32)
        self.tt(cre.ap, cre.b, cs.ap, are.ap, [cs.b, are.b], ALU.mult)
        self.tt(t1.ap, t1.b, sn.ap, aim.ap, [sn.b, aim.b], ALU.mult)
        self.tt(cre.ap, cre.b, cre.ap, t1.ap, [cre.b, t1.b], ALU.add)
        self.tt(cre.ap, cre.b, cre.ap, den.ap, [cre.b, den.b], ALU.mult)
        self.tt(cim.ap, cim.b, sn.ap, are.ap, [sn.b, are.b], ALU.mult)
        self.tt(t1.ap, t1.b, cs.ap, aim.ap, [cs.b, aim.b], ALU.mult)
        self.tt(cim.ap, cim.b, cim.ap, t1.ap, [cim.b, t1.b], ALU.subtract)
        self.tt(cim.ap, cim.b, cim.ap, den.ap, [cim.b, den.b], ALU.mult)
        bre = are
        bim = aim
        self.ld(bre, self.btre_d[l].rearrange("g r q -> r g q"))
        self.ld(bim, self.btim_d[l].rearrange("g r q -> r g q"))
        ta = sn
        tb_ = cs
        self.tt(ta.ap, ta.b, cre.ap, bre.ap, [cre.b, bre.b], ALU.mult)
        self.tt(tb_.ap, tb_.b, cim.ap, bim.ap, [cim.b, bim.b], ALU.mult)
        self.tt(wbx[:, :, 0, :], wbx.b, v3(ta), v3(tb_), [ta.b, tb_.b], ALU.subtract)
        self.tt(wbx[:, :, 3, :], wbx.b, v3(ta), v3(tb_), [ta.b, tb_.b], ALU.subtract)
        self.tt(ta.ap, ta.b, cre.ap, bim.ap, [cre.b, bim.b], ALU.mult)
        self.tt(tb_.ap, tb_.b, cim.ap, bre.ap, [cim.b, bre.b], ALU.mult)
        self.tt(wbx[:, :, 1, :], wbx.b, v3(ta), v3(tb_), [ta.b, tb_.b], ALU.add)
        self.stt(wbx[:, :, 2, :], wbx.b, v3(ta), -1.0, v3(tb_), ALU.mult, ALU.subtract, [ta.b, tb_.b])
        self.ld(bre, self.ctre_d[l].rearrange("g r q -> r g q"))
        self.ld(bim, self.ctim_d[l].rearrange("g r q -> r g q"))
        self.cp(cx[:, :, 0, :], cx.b, v3(bre), [bre.b])
        self.ts(cx[:, :, 1, :], cx.b, v3(bim), -1.0, None, ALU.mult, ALU.bypass, [bim.b])
        self.cp(cx[:, :, 2, :], cx.b, v3(bim), [bim.b])
        S.barrier()
        self.aoff = mark
        NS = 8
        ub32 = [self.alloc("ub32_%d" % i, [4, 128], F32) for i in range(2)]
        ub = [self.alloc("ub%d" % i, [4, 128], BF16) for i in range(2)]
        t1s = [self.alloc("t1s%d" % i, [2, 128], F32) for i in range(NS)]
        t2s = [self.alloc("t2s%d" % i, [2, 128], F32) for i in range(NS)]
        bp = [self.alloc("bp%d" % i, [2, 128], F32) for i in range(NS)]
        yb = [self.alloc("ybuf%d" % i, [2, 128], F32) for i in range(NS)]
        y1 = [self.alloc("y1_%d" % i, [2, 128], BF16) for i in range(NS)]
        y2 = [self.alloc("y2_%d" % i, [2, 128], BF16) for i in range(NS)]
        cy = self.alloc("cy", [16, 2], F32)
        cyt = self.alloc("cyt", [16, 2], F32)
        ylast = self.alloc("ylast", [16, 2], F32)
        og = [self.alloc("og%d" % i, [128], F32) for i in range(2)]
        ob = [self.alloc("ob%d" % i, [128], BF16) for i in range(2)]
        Uv = self.U.rearrange("(c p) n -> p c n", p=128)
        k = 0
        for sb in range(2):
            S.op("vector", "memset", cy.ap, 0.0, writes=[cy.b])
            for tc in range(16):
                ncol = slice(sb * SEQ + tc * 128, sb * SEQ + (tc + 1) * 128)
                u32 = ub32[tc % 2]
                u16 = ub[tc % 2]
                self.ld(u32, Uv[:, :, ncol])
                self.act(u16.ap, u16.b, u32.ap, [u32.b], AF.Copy)
                for uc in range(4):
                    sl0 = (k % 2) * 4
                    k += 1
                    gps = [uc * 4 + g4 for g4 in range(4)]
                    pbus = []
                    for g4, gp in enumerate(gps):
                        pbu = self.psum()
                        pbus.append(pbu)
                        for v in range(4):
                            self.mm(pbu[:, v * 128:(v + 1) * 128], pbu.b, wbx[:, gp, v, :], [wbx.b], u16[:, uc, :], [u16.b])
                    for g4, gp in enumerate(gps):
                        T1, T2, pbu = t1s[sl0 + g4], t2s[sl0 + g4], pbus[g4]
                        cbt = ctab[:, gp, :].unsqueeze(1).to_broadcast([128, 2, 128])
                        sbt = stab[:, gp, :].unsqueeze(1).to_broadcast([128, 2, 128])
                        self.tt(T1.ap, T1.b, pbu[:, 0:256].rearrange("p (a b) -> p a b", a=2), cbt, [pbu.b, ctab.b], ALU.mult)
                        self.tt(T2.ap, T2.b, pbu[:, 256:512].rearrange("p (a b) -> p a b", a=2), sbt, [pbu.b, stab.b], ALU.mult)
                    for g4, gp in enumerate(gps):
                        T1, T2, BP = t1s[sl0 + g4], t2s[sl0 + g4], bp[sl0 + g4]
                        self.tt(BP.ap, BP.b, T1.ap, T2.ap, [T1.b, T2.b], ALU.add, eng="gpsimd")
                    for g4, gp in enumerate(gps):
                        BP, Y = bp[sl0 + g4], yb[sl0 + g4]
                        mgb = magc[:, gp:gp + 1].to_broadcast([128, 128])
                        for ri in range(2):
                            S.op("vector", "tensor_tensor_scan", Y[:, ri, :], mgb, BP[:, ri, :], cy[:, gp, ri:ri + 1],
                                 ALU.mult, ALU.add, reads=[magc.b, BP.b, cy.b], writes=[Y.b])
                    for g4, gp in enumerate(gps):
                        Y, Y1, Y2 = yb[sl0 + g4], y1[sl0 + g4], y2[sl0 + g4]
                        cbt = ctab[:, gp, :].unsqueeze(1).to_broadcast([128, 2, 128])
                        sbt = stab[:, gp, :].unsqueeze(1).to_broadcast([128, 2, 128])
                        self.tt(Y1.ap, Y1.b, Y.ap, cbt, [Y.b, ctab.b], ALU.mult)
                        self.tt(Y2.ap, Y2.b, Y.ap, sbt, [Y.b, stab.b], ALU.mult, eng="gpsimd")
                        self.cp(ylast[:, gp, :], ylast.b, Y[:, :, 127], [Y.b], eng="gpsimd")
                    po = self.psum()
                    for g4, gp in enumerate(gps):
                        Y1, Y2 = y1[sl0 + g4], y2[sl0 + g4]
                        prs = ((0, Y1, 0), (0, Y2, 1), (1, Y1, 1), (2, Y2, 0))
                        for n_, (cv, yt_, ri) in enumerate(prs):
                            self.mm(po[:, 0:128], po.b, cx[:, gp, cv, :], [cx.b], yt_[:, ri, :], [yt_.b],
                                    start=(g4 == 0 and n_ == 0), stop=(g4 == 3 and n_ == 3))
                    o32 = og[uc % 2]
                    o16 = ob[uc % 2]
                    self.stt(o32.ap, o32.b, u32[:, uc, :], cols[:, C_DSK + uc:C_DSK + uc + 1], po[:, 0:128], ALU.mult, ALU.add,
                             [u32.b, cols.b, po.b])
                    self.act(o16.ap, o16.b, o32.ap, [o32.b], AF.Gelu)
                    self.stq(self.YB[uc * 128:(uc + 1) * 128, ncol], o16.ap, o16.b)
                c1b = c1.ap.unsqueeze(2).to_broadcast([128, 16, 2])
                s1b = s1.ap.unsqueeze(2).to_broadcast([128, 16, 2])
                self.tt(cy.ap, cy.b, ylast.ap, c1b, [ylast.b, c1.b], ALU.mult)
                self.tt(cyt.ap, cyt.b, ylast.ap, s1b, [ylast.b, s1.b], ALU.mult)
                self.tt(cy[:, :, 0:1], cy.b, cy[:, :, 0:1], cyt[:, :, 1:2], [cy.b, cyt.b], ALU.add)
                self.tt(cy[:, :, 1:2], cy.b, cy[:, :, 1:2], cyt[:, :, 0:1], [cy.b, cyt.b], ALU.subtract)

    def phaseE1(self, l):
        S = self.S
        self.phase()
        cols = self.cols
        wpr = self.alloc("wpr", [4, D], BF16)
        wgl = self.alloc("wgl", [4, 2 * D], BF16)
        wo = self.alloc("wo", [8, D], BF16)
        self.ldc(wpr, self.wproj_d[l].rearrange("(c p) n -> p c n", p=128))
        self.ldc(wgl, self.wglu_d[l].rearrange("(c p) n -> p c n", p=128))
        self.ldc(wo, self.wout_d[l].rearrange("(c p) n -> p c n", p=128))
        yt = self.alloc("yt", [4, 512], F32)
        bon = self.alloc("bon", [4, 512], F32)
        gg = self.alloc("gg", [4, 512], BF16)
        ybx = self.alloc("ybx", [4, 512], BF16)
        gts = self.alloc("gts", [16, 512], BF16)
        xt = self.alloc("xt", [8, 512], F32)
        mix = self.alloc("mix", [8, 512], BF16)
        ya = self.alloc("ya", [4, 512], BF16)
        y16 = [self.alloc("y16_%d" % i, [512], BF16) for i in range(4)]
        yc = [self.alloc("yc%d" % i, [512], F32) for i in range(4)]
        sq = [self.alloc("sq%d" % i, [512], BF16) for i in range(4)]
        rs = [self.alloc("rs%d" % i, [512], F32) for i in range(4)]
        m1 = [self.alloc("m1_%d" % i, [512], F32) for i in range(2)]
        sg = [self.alloc("sg%d" % i, [512], F32) for i in range(2)]
        m2 = [self.alloc("m2_%d" % i, [512], F32) for i in range(2)]
        v4 = lambda a: a.rearrange("(c p) n -> p c n", p=128)
        for ti in range(8):
            n0 = ti * 512
            ns = slice(n0, n0 + 512)
            self.ld(yt, v4(self.YT)[:, :, ns])
            self.ld(bon, v4(self.BON)[:, :, ns])
            self.ld(gg, v4(self.G)[:, :, ns])
            self.ld(ybx, v4(self.YB)[:, :, ns])
            self.ld(gts, v4(self.GT)[:, :, ns])
            self.ld(xt, v4(self.XT)[:, :, ns])
            for hp in range(4):
                self.act(y16[hp].ap, y16[hp].b, yt[:, hp, :], [yt.b], AF.Copy)
            for hp in range(4):
                p = self.psum()
                self.mm(p.ap, p.b, self.blkm.ap, [self.blkm.b], y16[hp].ap, [y16[hp].b])
                self.tt(yc[hp].ap, yc[hp].b, yt[:, hp, :], p.ap, [yt.b, p.b], ALU.subtract)
            for hp in range(4):
                self.act(sq[hp].ap, sq[hp].b, yc[hp].ap, [yc[hp].b], AF.Square)
            for hp in range(4):
                p = self.psum()
                self.mm(p.ap, p.b, self.blkm.ap, [self.blkm.b], sq[hp].ap, [sq[hp].b])
                self.act(rs[hp].ap, rs[hp].b, p.ap, [p.b], AF.Sqrt, bias=LNX_EPS)
            for hp in range(4):
                S.op("vector", "reciprocal", rs[hp].ap, rs[hp].ap, reads=[rs[hp].b], writes=[rs[hp].b])
            for hp in range(4):
                self.tt(yc[hp].ap, yc[hp].b, yc[hp].ap, rs[hp].ap, [yc[hp].b, rs[hp].b], ALU.mult)
            for hp in range(4):
                self.ts(yc[hp].ap, yc[hp].b, yc[hp].ap, cols[:, C_LW + hp:C_LW + hp + 1], cols[:, C_LB + hp:C_LB + hp + 1],
                        ALU.mult, ALU.add, [yc[hp].b, cols.b])
            for hp in range(4):
                self.tt(yc[hp].ap, yc[hp].b, yc[hp].ap, bon[:, hp, :], [yc[hp].b, bon.b], ALU.add, eng="gpsimd")
            for hp in range(4):
                self.tt(ya[:, hp, :], ya.b, yc[hp].ap, gg[:, hp, :], [yc[hp].b, gg.b], ALU.mult)
            for oc in range(8):
                osl = slice(oc * 128, (oc + 1) * 128)
                pa = self.psum()
                for hp in range(4):
                    self.mm(pa.ap, pa.b, wpr[:, hp, osl], [wpr.b], ya[:, hp, :], [ya.b], hp == 0, hp == 3)
                a1 = m1[oc % 2]
                self.tt(a1.ap, a1.b, pa.ap, gts[:, oc, :], [pa.b, gts.b], ALU.mult)
                pza = self.psum()
                for uc in range(4):
                    self.mm(pza.ap, pza.b, wgl[:, uc, osl], [wgl.b], ybx[:, uc, :], [ybx.b], uc == 0, uc == 3)
                pzb = self.psum()
                for uc in range(4):
                    self.mm(pzb.ap, pzb.b, wgl[:, uc, D + oc * 128:D + (oc + 1) * 128], [wgl.b], ybx[:, uc, :], [ybx.b],
                            uc == 0, uc == 3)
                s_ = sg[oc % 2]
                self.act(s_.ap, s_.b, pzb.ap, [pzb.b], AF.Sigmoid, extra=[cols.b],
                         bias=cols[:, C_BGLU + 8 + oc:C_BGLU + 8 + oc + 1])
                a2 = m2[oc % 2]
                self.stt(a2.ap, a2.b, pza.ap, cols[:, C_BGLU + oc:C_BGLU + oc + 1], s_.ap, ALU.add, ALU.mult,
                         [pza.b, cols.b, s_.b])
                self.tt(a2.ap, a2.b, a2.ap, gts[:, 8 + oc, :], [a2.b, gts.b], ALU.mult, eng="gpsimd")
                self.tt(mix[:, oc, :], mix.b, a1.ap, a2.ap, [a1.b, a2.b], ALU.add, eng="gpsimd")
            for dc in range(8):
                p = self.psum()
                for oc in range(8):
                    self.mm(p.ap, p.b, wo[:, oc, dc * 128:(dc + 1) * 128], [wo.b], mix[:, oc, :], [mix.b], oc == 0, oc == 7)
                self.tt(xt[:, dc, :], xt.b, xt[:, dc, :], p.ap, [xt.b, p.b], ALU.add)
            self.stq(v4(self.X1)[:, :, ns], xt.ap, xt.b)

    def phaseE2(self, l):
        S = self.S
        self.phase()
        NT = 256
        wup = [self.alloc("wup%d" % c, [DFF], BF16) for c in range(8)]
        wdn = [self.alloc("wdn%d" % c, [4, D], BF16) for c in range(8)]
        for c in range(8):
            self.ldc(wup[c], self.wup_d[l, c * 128:(c + 1) * 128, :])
        for c in range(8):
            self.ldc(wdn[c], self.wdn_d[l, c * 512:(c + 1) * 512, :].rearrange("(f p) n -> p f n", p=128))
        xts = [self.alloc("xt%d" % i, [8, NT], F32) for i in range(2)]
        sq = self.alloc("sq", [8, NT], BF16)
        xn = self.alloc("xn", [8, NT], BF16)
        rs = self.alloc("rs", [NT], F32)
        hT = self.alloc("hT", [32, NT], BF16)
        hTb = [Buf("hT%d" % i) for i in range(4)]
        rlu = [self.alloc("rlu%d" % i, [NT], F32) for i in range(2)]
        v4 = lambda a: a.rearrange("(c p) n -> p c n", p=128)
        for ti in range(NTOK // NT):
            ns = slice(ti * NT, (ti + 1) * NT)
            xt = xts[ti % 2]
            self.ld(xt, v4(self.X1)[:, :, ns])
            self.rmsnorm(xt, NT, C_G2, self.cols, xn, sq, rs)
            for fc in range(32):
                p = self.psum()
                for c in range(8):
                    self.mm(p[:, 0:NT], p.b, wup[c][:, fc * 128:(fc + 1) * 128], [wup[c].b], xn[:, c, :], [xn.b], c == 0, c == 7)
                rl = rlu[fc % 2]
                self.act(rl.ap, rl.b, p[:, 0:NT], [p.b], AF.Relu)
                self.tt(hT[:, fc, :], hTb[fc % 4], rl.ap, rl.ap, [rl.b], ALU.mult, eng=("vector" if fc % 2 == 0 else "gpsimd"))
            for dc in range(8):
                p = self.psum()
                for fc in range(32):
                    self.mm(p[:, 0:NT], p.b, wdn[fc // 4][:, fc % 4, dc * 128:(dc + 1) * 128], [wdn[fc // 4].b],
                            hT[:, fc, :], [hTb[fc % 4]], fc == 0, fc == 31)
                self.tt(xt[:, dc, :], xt.b, xt[:, dc, :], p[:, 0:NT], [xt.b, p.b], ALU.add)
            self.stq(v4(self.XT)[:, :, ns], xt.ap, xt.b)

    def phaseF(self):
        self.phase()
        xts = [self.alloc("xt%d" % i, [8, 512], F32) for i in range(2)]
        sq = self.alloc("sq", [8, 512], BF16)
        xn = self.alloc("xn", [8, 512], F32)
        rs = self.alloc("rs", [512], F32)
        oo = [self.alloc("oo%d" % i, [1024], F32) for i in range(2)]
        v4 = lambda a: a.rearrange("(c p) n -> p c n", p=128)
        k = 0
        for ti in range(8):
            xt = xts[ti % 2]
            self.ld(xt, v4(self.XT)[:, :, ti * 512:(ti + 1) * 512])
            self.rmsnorm(xt, 512, 0, self.gf, xn, sq, rs)
            for tb in range(4):
                o = oo[k % 2]
                k += 1
                for h in range(2):
                    p = self.psum()
                    for j in range(4):
                        c = h * 4 + j
                        self.S.op("tensor", "transpose", p[:, j * 128:(j + 1) * 128], xn[:, c, tb * 128:(tb + 1) * 128],
                                  self.idf.ap, reads=[xn.b, self.idf.b], writes=[p.b])
                    self.cp(o[:, h * 512:(h + 1) * 512], o.b, p.ap, [p.b], eng=("vector" if h == 0 else "scalar"))
                r0 = ti * 512 + tb * 128
                self.stq(self.out[r0:r0 + 128, :], o.ap, o.b)


_CACHE = {}


def _get_nc(nlayers):
    if nlayers not in _CACHE:
        _CACHE[nlayers] = KB(nlayers).build()
    return _CACHE[nlayers]


def _pack(inp, L):
    f = lambda a: np.ascontiguousarray(np.asarray(a, dtype=np.float32))
    inp = {k: (np.asarray(v)[:L] if k not in ("x", "norm_f_g") else v) for k, v in inp.items()}
    cols = np.zeros((L, 128, NCOLS), np.float32)
    pc = lambda v, n: f(v).reshape(L, n, 128).transpose(0, 2, 1)
    cols[:, :, C_MU:C_MU + 14] = pc(inp["mu_shift"], 14)
    cols[:, :, C_KK:C_KK + 4] = pc(inp["k_k"], 4)
    cols[:, :, C_KA:C_KA + 4] = pc(inp["k_a"], 4)
    cols[:, :, C_RK:C_RK + 4] = pc(f(inp["r_k"]).reshape(L, 512), 4)
    cols[:, :, C_A0:C_A0 + 4] = pc(inp["a0"], 4)
    cols[:, :, C_LW:C_LW + 4] = pc(inp["lnx_w"], 4)
    cols[:, :, C_LB:C_LB + 4] = pc(inp["lnx_b"], 4)
    cols[:, :, C_DSK:C_DSK + 4] = pc(inp["d_skip"], 4)
    cols[:, :, C_BGLU:C_BGLU + 16] = pc(inp["b_glu"], 16)
    cols[:, :, C_G1:C_G1 + 8] = pc(inp["norm1_g"], 8)
    cols[:, :, C_G2:C_G2 + 8] = pc(inp["norm2_g"], 8)
    pg = lambda v: f(v).reshape(L, 16, 2, 64).transpose(0, 2, 3, 1).reshape(L, 128, 16)
    cols[:, :, C_ARE:C_ARE + 16] = pg(inp["a_re"])
    cols[:, :, C_AIM:C_AIM + 16] = pg(inp["a_im"])
    ldt = np.repeat(f(inp["log_dt"])[:, :, None], 64, axis=2)
    cols[:, :, C_LDT:C_LDT + 16] = pg(ldt)
    bt = {}
    for nm, src in (("btre", inp["b_re"]), ("btim", inp["b_im"])):
        a = f(src).reshape(L, 16, 2, 64, 16)
        o = np.zeros((L, 16, 128, 128), np.float32)
        for e in range(2):
            for m in range(4):
                o[:, m::4, 32 * m + 16 * e:32 * m + 16 * e + 16, 64 * e:64 * e + 64] = \
                    a[:, m::4, e].transpose(0, 1, 3, 2)
        bt[nm] = o
    for nm, src in (("ctre", inp["c_re"]), ("ctim", inp["c_im"])):
        a = f(src).reshape(L, 16, 2, 16, 64)
        o = np.zeros((L, 16, 128, 128), np.float32)
        for e in range(2):
            for m in range(4):
                o[:, m::4, 64 * e:64 * e + 64, 32 * m + 16 * e:32 * m + 16 * e + 16] = \
                    a[:, m::4, e].transpose(0, 1, 3, 2)
        bt[nm] = o
    shared = {
        "w_in": f(inp["w_in"])[:L], "cols": cols,
        "gf": f(inp["norm_f_g"]).reshape(8, 128).T.copy(),
        "w0": f(inp["w0"])[:L], "wdu": f(inp["w_decay_up"])[:L], "wau": f(inp["w_aaa_up"])[:L],
        "wgu": f(inp["w_gate_up"])[:L], "wproj": f(inp["w_rwkv_proj"])[:L], "wglu": f(inp["w_glu"])[:L],
        "wout": f(inp["w_out"])[:L], "wup": f(inp["w_ff_up"])[:L], "wdn": f(inp["w_ff_down"])[:L],
        "areR": f(inp["a_re"]).reshape(-1, 2048)[:L], "aimR": f(inp["a_im"]).reshape(-1, 2048)[:L],
        "ldtR": ldt.reshape(-1, 2048)[:L],
    }
    shared.update({k: v[:L] for k, v in bt.items()})
    su = np.triu(np.ones((128, 128), np.float32), 1)
    iu = np.triu(np.ones((128, 128), np.float32), 0)
    sl = np.tril(np.ones((128, 128), np.float32), -1)
    shared["c_ms2"] = np.concatenate([su, iu], 1)
    shared["c_msl"] = sl
    shared["c_trie"] = np.float32(-DS) * np.concatenate([iu, su], 1)
    shared["c_trir"] = np.float32(-DS) * sl
    return shared


def kernel(**inputs):
    L = DEPTH
    nc = _get_nc(L)
    shared = _pack(inputs, L)
    x = np.asarray(inputs["x"], dtype=np.float32)
    in_maps = []
    for c in range(8):
        m = dict(shared)
        m["x"] = np.ascontiguousarray(x[2 * c:2 * c + 2].reshape(NTOK, D))
        in_maps.append(m)
    res = run_bass_kernel_spmd(nc, in_maps, core_ids=list(range(8)))
    out = np.stack([np.asarray(r["out"]).reshape(2, SEQ, D) for r in res.results], axis=0)
    return out.reshape(16, SEQ, D).astype(np.float32)
```
